# Optimizing a Trainium2 kernel written in Bass

```python
import math
import jax, jax.numpy as jnp
from jax import lax
import numpy as np

D_MODEL = 1024
BATCH = 8
SEQ = 8192
DEPTH = 2

HEAD_DIM = 64
ROT_DIM = HEAD_DIM // 4
ROPE_THETA = 500000.0
DIFF_HEADS = 4
DIFF_VDIM = 2 * HEAD_DIM
DIFF_QK_WIDTH = DIFF_HEADS * 2 * HEAD_DIM
DIFF_V_WIDTH = DIFF_HEADS * DIFF_VDIM
DIL_HEADS = 8
DIL_WIDTH = DIL_HEADS * HEAD_DIM
DIL_PATTERNS = ((128, 1), (512, 4), (2048, 16))
RNN_WIDTH = D_MODEL
LRU_BLOCKS = 16
LRU_BLOCK = RNN_WIDTH // LRU_BLOCKS
LRU_C = 8.0
CONV_WIDTH = 4
N_BRANCHES = 3
Q_BLOCK = 128
MEM_TOKENS = 256
XATTN_HEADS = 4
XATTN_HEAD_DIM = D_MODEL // XATTN_HEADS
FFN_HIDDEN = -(-8 * D_MODEL // (3 * 256)) * 256
DEEPNORM_ALPHA = (2 * DEPTH) ** 0.25
DEEPNORM_BETA = (8 * DEPTH) ** -0.25
LN_EPS = 1e-5
IN_SPLITS = (DIFF_QK_WIDTH, DIFF_QK_WIDTH, DIFF_V_WIDTH, RNN_WIDTH, RNN_WIDTH,
             DIL_WIDTH, DIL_WIDTH, DIL_WIDTH, N_BRANCHES * D_MODEL)
IN_WIDTH = sum(IN_SPLITS)

kernel_name = "hybrid_diffattn_rglru_dilated_deepnorm"


def layer_norm(x, g, b):
    xf = x.astype(jnp.float32)
    mu = jnp.mean(xf, axis=-1, keepdims=True)
    var = jnp.mean(jnp.square(xf - mu), axis=-1, keepdims=True)
    y = (xf - mu) * lax.rsqrt(var + LN_EPS)
    return (y * g.astype(jnp.float32) + b.astype(jnp.float32)).astype(x.dtype)


def rms_norm(x, g):
    xf = x.astype(jnp.float32)
    y = xf * lax.rsqrt(jnp.mean(xf * xf, axis=-1, keepdims=True) + LN_EPS)
    return (y * g.astype(jnp.float32)).astype(x.dtype)


def partial_rotary(x, pos):
    half = ROT_DIM // 2
    inv_freq = jnp.power(ROPE_THETA, -2.0 * jnp.arange(half, dtype=jnp.float32) / ROT_DIM)
    ang = pos.astype(jnp.float32)[:, None] * inv_freq[None, :]
    shape = (1, ang.shape[0]) + (1,) * (x.ndim - 3) + (half,)
    cos = jnp.cos(ang).reshape(shape)
    sin = jnp.sin(ang).reshape(shape)
    xf = x.astype(jnp.float32)
    x1 = xf[..., :half]
    x2 = xf[..., half:ROT_DIM]
    out = jnp.concatenate([x1 * cos - x2 * sin, x2 * cos + x1 * sin, xf[..., ROT_DIM:]], axis=-1)
    return out.astype(x.dtype)


def diff_attention(q, k, v, lam, subln_g, lambda_init):
    B, S = q.shape[0], q.shape[1]
    nb = S // Q_BLOCK
    scale = HEAD_DIM ** -0.5
    kpos = jnp.arange(S)
    qb = q.reshape(B, nb, Q_BLOCK, DIFF_HEADS, 2, HEAD_DIM).transpose(1, 0, 2, 3, 4, 5)
    starts = jnp.arange(nb) * Q_BLOCK

    def block(args):
        qblk, start = args
        s = jnp.einsum('bqhcd,bkhcd->bhcqk', qblk, k).astype(jnp.float32) * scale
        causal = (start + jnp.arange(Q_BLOCK))[:, None] >= kpos[None, :]
        p = jax.nn.softmax(jnp.where(causal, s, -jnp.inf), axis=-1)
        w = p[:, :, 0] - lam * p[:, :, 1]
        return jnp.einsum('bhqk,bkhe->bqhe', w.astype(v.dtype), v)

    o = lax.map(block, (qb, starts))
    o = o.transpose(1, 0, 2, 3, 4).reshape(B, S, DIFF_HEADS, DIFF_VDIM)
    o = rms_norm(o, subln_g) * (1.0 - lambda_init)
    return o.reshape(B, S, DIFF_V_WIDTH)


def dilated_band_attention(q, k, v, dil, steps):
    B, S, H, Dh = q.shape
    pad = (-S) % (dil * steps)
    Sp = S + pad
    M = Sp // dil
    nb = M // steps

    def to_blocks(t):
        t = jnp.pad(t, ((0, 0), (0, pad), (0, 0), (0, 0)))
        t = t.reshape(B, M, dil, H, Dh).transpose(0, 2, 1, 3, 4)
        return t.reshape(B, dil, nb, steps, H, Dh)

    qb, kb, vb = to_blocks(q), to_blocks(k), to_blocks(v)
    zpad = ((0, 0), (0, 0), (1, 0), (0, 0), (0, 0), (0, 0))
    kk = jnp.concatenate([jnp.pad(kb, zpad)[:, :, :-1], kb], axis=3)
    vv = jnp.concatenate([jnp.pad(vb, zpad)[:, :, :-1], vb], axis=3)
    s = jnp.einsum('bpnqhd,bpnkhd->bpnhqk', qb, kk).astype(jnp.float32) * (Dh ** -0.5)
    i = jnp.arange(steps)[:, None]
    j = jnp.arange(2 * steps)[None, :]
    dist = steps + i - j
    band = (dist >= 0) & (dist <= steps)
    valid = band[None] & ((jnp.arange(nb)[:, None, None] > 0) | (j >= steps)[None])
    s = jnp.where(valid[None, None, :, None], s, -jnp.inf)
    lse = jax.nn.logsumexp(s, axis=-1)
    p = jnp.exp(s - lse[..., None])
    o = jnp.einsum('bpnhqk,bpnkhd->bpnqhd', p.astype(v.dtype), vv)
    o = o.reshape(B, dil, M, H, Dh).transpose(0, 2, 1, 3, 4).reshape(B, Sp, H, Dh)[:, :S]
    lse = lse.transpose(0, 1, 2, 4, 3).reshape(B, dil, M, H).transpose(0, 2, 1, 3).reshape(B, Sp, H)[:, :S]
    return o, lse


def dilated_attention(q, k, v):
    outs, lses = [], []
    for window, dil in DIL_PATTERNS:
        o, lse = dilated_band_attention(q, k, v, dil, window // dil)
        outs.append(o.astype(jnp.float32))
        lses.append(lse)
    wts = jax.nn.softmax(jnp.stack(lses, axis=0), axis=0)
    o = jnp.sum(wts[..., None] * jnp.stack(outs, axis=0), axis=0)
    B, S = q.shape[0], q.shape[1]
    return o.reshape(B, S, DIL_WIDTH).astype(q.dtype)


def rg_lru_branch(xr, gate, conv_w, conv_b, gate_w, gate_b, lru_lambda):
    B, S, _ = xr.shape
    xp = jnp.pad(xr, ((0, 0), (CONV_WIDTH - 1, 0), (0, 0)))
    xc = conv_b
    for tap in range(CONV_WIDTH):
        xc = xc + xp[:, tap:tap + S] * conv_w[tap]
    xb = xc.reshape(B, S, LRU_BLOCKS, LRU_BLOCK)
    gates = jnp.einsum('bsnc,gncd->gbsnd', xb, gate_w).reshape(2, B, S, RNN_WIDTH) + gate_b[:, None, None, :]
    r = jax.nn.sigmoid(gates[0].astype(jnp.float32))
    i = jax.nn.sigmoid(gates[1].astype(jnp.float32))
    log_a = -LRU_C * r * jax.nn.softplus(-lru_lambda.astype(jnp.float32))
    a = jnp.exp(log_a)
    u = jnp.sqrt(-jnp.expm1(2.0 * log_a)) * (i * xc.astype(jnp.float32))

    def step(h, inp):
        a_t, u_t = inp
        h = a_t * h + u_t
        return h, h

    _, hs = lax.scan(step, jnp.zeros((B, RNN_WIDTH), jnp.float32),
                     (a.transpose(1, 0, 2), u.transpose(1, 0, 2)))
    h = hs.transpose(1, 0, 2).astype(xr.dtype)
    return h * jax.nn.gelu(gate)


def hybrid_mixer(x, layer, w_in, lam_qk, diff_subln, conv_w, conv_b, gate_w, gate_b, lru_lambda,
                 w_br_a, w_br_b, w_br_c, w_out):
    B, S, _ = x.shape
    pos = jnp.arange(S)
    u = x @ w_in
    offsets = np.cumsum(IN_SPLITS)[:-1].tolist()
    aq, ak, av, rx, rg, cq, ck, cv, gl = jnp.split(u, offsets, axis=-1)
    lambda_init = 0.8 - 0.6 * math.exp(-0.3 * layer)
    lq = lam_qk.astype(jnp.float32)
    lam = jnp.exp(jnp.sum(lq[0] * lq[1])) - jnp.exp(jnp.sum(lq[2] * lq[3])) + lambda_init
    qa = partial_rotary(aq.reshape(B, S, DIFF_HEADS, 2, HEAD_DIM), pos)
    ka = partial_rotary(ak.reshape(B, S, DIFF_HEADS, 2, HEAD_DIM), pos)
    o_a = diff_attention(qa, ka, av.reshape(B, S, DIFF_HEADS, DIFF_VDIM), lam, diff_subln, lambda_init)
    o_b = rg_lru_branch(rx, rg, conv_w, conv_b, gate_w, gate_b, lru_lambda)
    qc = partial_rotary(cq.reshape(B, S, DIL_HEADS, HEAD_DIM), pos)
    kc = partial_rotary(ck.reshape(B, S, DIL_HEADS, HEAD_DIM), pos)
    o_c = dilated_attention(qc, kc, cv.reshape(B, S, DIL_HEADS, HEAD_DIM))
    g = jax.nn.sigmoid(gl.reshape(B, S, N_BRANCHES, D_MODEL))
    merged = g[:, :, 0] * (o_a @ w_br_a) + g[:, :, 1] * (o_b @ w_br_b) + g[:, :, 2] * (o_c @ w_br_c)
    return merged @ w_out


def memory_cross_attention(x, mem, wq, wkv, wo):
    B, S, _ = x.shape
    q = (x @ wq).reshape(B, S, XATTN_HEADS, XATTN_HEAD_DIM)
    k, v = jnp.split(mem @ wkv, 2, axis=-1)
    k = k.reshape(B, -1, XATTN_HEADS, XATTN_HEAD_DIM)
    v = v.reshape(B, -1, XATTN_HEADS, XATTN_HEAD_DIM)
    s = jnp.einsum('bqhd,bmhd->bhqm', q, k).astype(jnp.float32) * (XATTN_HEAD_DIM ** -0.5)
    p = jax.nn.softmax(s, axis=-1)
    o = jnp.einsum('bhqm,bmhd->bqhd', p.astype(v.dtype), v).reshape(B, S, D_MODEL)
    return o @ wo


def swiglu(x, w_up, w_down):
    gpart, upart = jnp.split(x @ w_up, 2, axis=-1)
    return (jax.nn.silu(gpart) * upart) @ w_down


def setup_inputs(seed: int = 0) -> dict:
    key = jax.random.key(seed)
    ks = iter(jax.random.split(key, 32))
    nrm = lambda shape, scale: jax.random.normal(next(ks), shape, jnp.float32) * scale
    L, D = DEPTH, D_MODEL
    u = jax.random.uniform(next(ks), (L, RNN_WIDTH), jnp.float32, minval=0.9, maxval=0.999)
    p = u ** (1.0 / LRU_C)
    lru_lambda = jnp.log(p) - jnp.log1p(-p)
    return {
        "x": nrm((BATCH, SEQ, D), 1.0),
        "mem": nrm((BATCH, MEM_TOKENS, D), 1.0),
        "w_in": nrm((L, D, IN_WIDTH), D ** -0.5),
        "lam_qk": nrm((L, 4, HEAD_DIM), 0.1),
        "diff_subln": 1.0 + nrm((L, DIFF_VDIM), 0.05),
        "conv_w": nrm((L, CONV_WIDTH, RNN_WIDTH), CONV_WIDTH ** -0.5),
        "conv_b": nrm((L, RNN_WIDTH), 0.01),
        "gate_w": nrm((L, 2, LRU_BLOCKS, LRU_BLOCK, LRU_BLOCK), LRU_BLOCK ** -0.5),
        "gate_b": nrm((L, 2, RNN_WIDTH), 0.01),
        "lru_lambda": lru_lambda,
        "w_br_a": nrm((L, DIFF_V_WIDTH, D), DIFF_V_WIDTH ** -0.5),
        "w_br_b": nrm((L, RNN_WIDTH, D), RNN_WIDTH ** -0.5),
        "w_br_c": nrm((L, DIL_WIDTH, D), DIL_WIDTH ** -0.5),
        "w_out": nrm((L, D, D), D ** -0.5 * DEEPNORM_BETA),
        "ln1_g": 1.0 + nrm((L, D), 0.02),
        "ln1_b": nrm((L, D), 0.02),
        "xq": nrm((L, D, D), D ** -0.5),
        "xkv": nrm((L, D, 2 * D), D ** -0.5),
        "xo": nrm((L, D, D), D ** -0.5 * DEEPNORM_BETA),
        "ln2_g": 1.0 + nrm((L, D), 0.02),
        "ln2_b": nrm((L, D), 0.02),
        "w_up": nrm((L, D, 2 * FFN_HIDDEN), D ** -0.5),
        "w_down": nrm((L, FFN_HIDDEN, D), FFN_HIDDEN ** -0.5 * DEEPNORM_BETA),
        "ln3_g": 1.0 + nrm((L, D), 0.02),
        "ln3_b": nrm((L, D), 0.02),
    }


def reference(x, mem, w_in, lam_qk, diff_subln, conv_w, conv_b, gate_w, gate_b, lru_lambda,
              w_br_a, w_br_b, w_br_c, w_out, ln1_g, ln1_b, xq, xkv, xo, ln2_g, ln2_b,
              w_up, w_down, ln3_g, ln3_b):
    for l in range(DEPTH):
        h = hybrid_mixer(x, l, w_in[l], lam_qk[l], diff_subln[l], conv_w[l], conv_b[l], gate_w[l],
                         gate_b[l], lru_lambda[l], w_br_a[l], w_br_b[l], w_br_c[l], w_out[l])
        x = layer_norm(DEEPNORM_ALPHA * x + h, ln1_g[l], ln1_b[l])
        x = layer_norm(DEEPNORM_ALPHA * x + memory_cross_attention(x, mem, xq[l], xkv[l], xo[l]),
                       ln2_g[l], ln2_b[l])
        x = layer_norm(DEEPNORM_ALPHA * x + swiglu(x, w_up[l], w_down[l]), ln3_g[l], ln3_b[l])
    return x
```

```python
import math
from contextlib import ExitStack

import numpy as np
import concourse.bass as bass
import concourse.mybir as mybir
from concourse.bass_utils import run_bass_kernel_spmd

F32 = mybir.dt.float32
BF16 = mybir.dt.bfloat16
U8 = mybir.dt.uint8
AF = mybir.ActivationFunctionType
ALU = mybir.AluOpType

D = 1024
NKC = 8
DEPTH = 2
FFN_H = 2816
IN_W = 8192
ALPHA = (2 * DEPTH) ** 0.25
LN_EPS = 1e-5
MEM_T = 256
N_CORES = 8
SEQ = 8192

SAME_ENGINE_SYNC = True
SEM_LIMIT = 30000


class Res:
    __slots__ = ("name", "writers", "readers", "excl")

    def __init__(self, name="", excl=False):
        self.name = name
        self.writers = []
        self.readers = []
        self.excl = excl


class Op:
    __slots__ = ("eng", "fn", "deps", "raw", "seq", "pos", "signal", "is_dma", "slot", "round", "tok", "cost",
                 "seg", "gid", "fin", "done", "nwait", "users", "ready", "tag")


SCHED = True
SEM_LAT = 400.0
DMA_LAT = 2500.0
DMA_BW = 150.0


class Prog:
    ENGS = ("pe", "act", "dve", "pool", "sp")
    REORDER = ("pe", "act", "dve")

    def __init__(self, nc, n_dma_slots=8):
        self.nc = nc
        self.ops = {e: [] for e in self.ENGS}
        self.all = []
        self.n_dma_slots = n_dma_slots
        self.dma_count = {e: 0 for e in self.ENGS}
        self.dma_by_slot = {}
        self.last_compute = {e: None for e in self.ENGS}
        self.seg = 0
        self.barrier_deps = {0: []}

    def add(self, eng, fn, reads=(), writes=(), partial=(), dma=False, cost=200.0, tag=None):
        op = Op()
        op.tag = tag
        op.eng = eng
        op.fn = fn
        op.is_dma = dma
        op.signal = False
        op.seq = len(self.ops[eng])
        op.gid = len(self.all)
        op.tok = None
        op.cost = cost
        op.seg = self.seg
        deps = {}
        for r in reads:
            for d in r.writers:
                deps[d.gid] = d
            if r.excl:
                for d in r.readers:
                    deps[d.gid] = d
        for w in writes:
            for d in w.writers:
                deps[d.gid] = d
            for d in w.readers:
                deps[d.gid] = d
        for w in partial:
            for d in w.writers:
                deps[d.gid] = d
            for d in w.readers:
                deps[d.gid] = d
        if dma:
            k = self.dma_count[eng]
            self.dma_count[eng] += 1
            op.slot = k % self.n_dma_slots
            op.round = k // self.n_dma_slots
            prev = self.dma_by_slot.get((eng, op.slot))
            if prev is not None:
                deps[prev.gid] = prev
            self.dma_by_slot[(eng, op.slot)] = op
        op.raw = [d for d in deps.values() if d.seg == op.seg]
        self.ops[eng].append(op)
        self.all.append(op)
        if not dma:
            self.last_compute[eng] = op
        for r in reads:
            r.readers.append(op)
        for w in writes:
            w.writers = [op]
            w.readers = []
        for w in partial:
            w.writers.append(op)
            w.readers = []
        return op

    def dma(self, q, out, in_, reads=(), writes=(), partial=(), **kw):
        nbytes = 1
        for s_ in out.shape:
            nbytes *= s_
        nbytes *= 4 if out.dtype == F32 else 2
        return self.add(q, lambda e: e.dma_start(out=out, in_=in_, **kw), reads, writes, partial, dma=True,
                        cost=float(nbytes))

    def barrier(self):
        self.seg += 1
        self.barrier_deps[self.seg] = list(self.dma_by_slot.values())

    def schedule(self):
        order = {e: [] for e in self.ENGS}
        free = {e: 0.0 for e in self.ENGS}
        segs = {}
        for op in self.all:
            segs.setdefault(op.seg, []).append(op)
        for sg in sorted(segs):
            ops = segs[sg]
            t0 = max(free.values())
            for e in self.ENGS:
                free[e] = t0
            for op in ops:
                op.done = False
                op.users = []
                op.ready = t0
            for op in ops:
                op.nwait = len(op.raw)
                for d in op.raw:
                    d.users.append(op)
            heads = {e: 0 for e in ("pool", "sp")}
            inord = {e: [o for o in ops if o.eng == e] for e in ("pool", "sp")}
            rel = {e: [] for e in self.REORDER}
            for op in ops:
                if op.nwait == 0 and op.eng in rel:
                    rel[op.eng].append(op)
            remaining = len(ops)
            last_tag = None
            while remaining:
                best = None
                bstart = None
                for e in self.REORDER:
                    lst = rel[e]
                    if not lst:
                        continue
                    f = free[e]
                    c = None
                    cs = None
                    lt = last_tag if e == "act" else None
                    for o in lst:
                        st = o.ready if o.ready > f else f
                        if lt is not None and o.tag is not None and o.tag != lt:
                            st += 1000.0
                        if c is None or st < cs or (st == cs and o.gid < c.gid):
                            c, cs = o, st
                    if best is None or cs < bstart:
                        best, bstart = c, cs
                for e in ("pool", "sp"):
                    i = heads[e]
                    if i < len(inord[e]):
                        o = inord[e][i]
                        if o.nwait == 0:
                            st = o.ready if o.ready > free[e] else free[e]
                            if best is None or st < bstart:
                                best, bstart = o, st
                op = best
                e = op.eng
                if e == "act" and op.tag is not None:
                    last_tag = op.tag
                if e in rel:
                    rel[e].remove(op)
                else:
                    heads[e] += 1
                if op.is_dma:
                    issue = 1200.0 if e == "pool" else 150.0
                    free[e] = bstart + issue
                    op.fin = bstart + issue + DMA_LAT + op.cost / DMA_BW
                else:
                    op.fin = bstart + op.cost
                    free[e] = op.fin
                op.done = True
                op.pos = len(order[e])
                order[e].append(op)
                remaining -= 1
                for u in op.users:
                    u.nwait -= 1
                    r = op.fin + (0.0 if (u.eng == e and not op.is_dma) else SEM_LAT)
                    if r > u.ready:
                        u.ready = r
                    if u.nwait == 0 and u.eng in rel:
                        rel[u.eng].append(u)
        return order

    def emit(self, es, final_ops=()):
        nc = self.nc
        if SCHED:
            order = self.schedule()
        else:
            order = {e: list(self.ops[e]) for e in self.ENGS}
            for e in self.ENGS:
                for i, o in enumerate(order[e]):
                    o.pos = i
        nseg = self.seg + 1
        last_le = {}
        for f in self.ENGS:
            arr = [None] * (nseg + 1)
            for o in order[f]:
                if not o.is_dma:
                    arr[o.seg] = o
            for s_ in range(1, nseg + 1):
                if arr[s_] is None:
                    arr[s_] = arr[s_ - 1]
            last_le[f] = arr
        final_ops = list(self.dma_by_slot.values()) + [last_le[f][nseg] for f in self.ENGS if last_le[f][nseg] is not None]
        for o in final_ops:
            o.signal = True
        for e in self.ENGS:
            seen = {f: -1 for f in self.ENGS}
            seen_dma = {}
            cur_seg = -1
            for op in order[e]:
                deps = list(op.raw)
                if op.seg != cur_seg:
                    if op.seg > 0:
                        deps.extend(self.barrier_deps.get(op.seg, []))
                        for f in self.ENGS:
                            lo = last_le[f][op.seg - 1]
                            if lo is not None:
                                deps.append(lo)
                    cur_seg = op.seg
                fin = []
                for d in deps:
                    if d is op:
                        continue
                    if d.is_dma:
                        key = (d.eng, d.slot)
                        if seen_dma.get(key, -1) >= d.round:
                            continue
                        seen_dma[key] = d.round
                        fin.append(d)
                        d.signal = True
                    else:
                        if d.eng == e and (e == "pe" or not SAME_ENGINE_SYNC):
                            continue
                        if seen[d.eng] >= d.pos:
                            continue
                        seen[d.eng] = d.pos
                        fin.append(d)
                        d.signal = True
                op.deps = fin
        sems = {}
        for e in self.ENGS:
            n_sig = sum(1 for o in order[e] if o.signal and not o.is_dma)
            n_sems = max(1, -(-n_sig // SEM_LIMIT))
            sems[e] = [es.enter_context(nc.semaphore(f"c_{e}_{i}")) for i in range(n_sems)]
            c = 0
            for o in order[e]:
                if o.signal and not o.is_dma:
                    o.tok = (sems[e][c // SEM_LIMIT], c % SEM_LIMIT + 1)
                    c += 1
        per = max(1, SEM_LIMIT // 16)
        for e in self.ENGS:
            if self.dma_count[e] == 0:
                continue
            rounds = -(-self.dma_count[e] // self.n_dma_slots)
            n_gen = -(-rounds // per)
            ds = [[es.enter_context(nc.semaphore(f"d_{e}_{s}_{g}")) for g in range(n_gen)]
                  for s in range(self.n_dma_slots)]
            for o in order[e]:
                if o.is_dma:
                    o.tok = (ds[o.slot][o.round // per], 16 * (o.round % per + 1))
        block = es.enter_context(nc.Block())
        engmap = {"pe": "tensor", "act": "scalar", "dve": "vector", "pool": "gpsimd", "sp": "sync"}
        for e in self.ENGS:
            ops = order[e]
            if not ops:
                continue

            def body(eng, ops=ops, e=e):
                for o in ops:
                    for d in o.deps:
                        eng.wait_ge(d.tok[0], d.tok[1])
                    ins = o.fn(eng)
                    if o.is_dma:
                        ins.then_inc(o.tok[0], 16)
                    elif o.signal:
                        ins.then_inc(o.tok[0], 1)
                if e == "sp":
                    for d in final_ops:
                        eng.wait_ge(d.tok[0], d.tok[1])

            getattr(block, engmap[e])(body)


class Arena:
    def __init__(self, tile, lo, hi):
        self.tile, self.lo, self.hi, self.off = tile, lo, hi, lo

    def reset(self):
        self.off = self.lo

    def alloc(self, shape, dt):
        esz = 4 if dt == F32 else 2
        n = 1
        for s in shape[1:]:
            n *= s
        nb = n * esz
        off = (self.off + 63) // 64 * 64
        assert off + nb <= self.hi, f"arena overflow {off + nb} > {self.hi}"
        self.off = off + nb
        v = self.tile[0:shape[0], off:off + nb].bitcast(dt)
        if len(shape) == 3:
            v = v.rearrange("p (a b) -> p a b", a=shape[1])
        elif len(shape) == 4:
            v = v.rearrange("p (a b c) -> p a b c", a=shape[1], b=shape[2])
        return v


def build(S=SEQ, L=DEPTH, dbg=(), phases=None):
    nc = bass.Bass("TRN2", target_bir_lowering=False)
    NT, NG = S // 128, S // 512
    allph = phases is None

    def want(name):
        return allph or name in phases

    def din(name, shape, dt=F32):
        return nc.dram_tensor(name, list(shape), dt, kind="ExternalInput").ap()

    def dscr(name, shape, dt):
        kind = "ExternalOutput" if name in dbg else "Internal"
        return nc.dram_tensor(name, list(shape), dt, kind=kind).ap()

    x_in = din("x", [S, D])
    mem = din("mem", [MEM_T, D])
    w_in = din("w_in", [L, D, IN_W])
    lam_qk = din("lam_qk", [L, 4 * 64])
    diff_subln = din("diff_subln", [L, 128])
    rnn_par = din("rnn_par", [L, 8, D])
    gate_w = din("gate_w", [L, 2, 16, 64, 64])
    w_br_a = din("w_br_a", [L, 512, D])
    w_br_b = din("w_br_b", [L, D, D])
    w_br_c = din("w_br_c", [L, 512, D])
    w_out = din("w_out", [L, D, D])
    lnp = din("lnp", [L, 6, D])
    xq = din("xq", [L, D, D])
    xkv = din("xkv", [L, D, 2 * D])
    xo = din("xo", [L, D, D])
    w_up = din("w_up", [L, D, 2 * FFN_H])
    w_down = din("w_down", [L, FFN_H, D])
    c_ident = din("c_ident", [128, 128])
    c_rot = din("c_rot", [S, 16])
    c_maskd = din("c_maskd", [128, 128])
    c_maskc = din("c_maskc", [128, 512])
    y_out = nc.dram_tensor("y", [S, D], F32, kind="ExternalOutput").ap()
    dbgbuf = nc.dram_tensor("dbgbuf", [16, 128, 512], F32, kind="ExternalOutput").ap() if "dbgbuf" in dbg else None
    r_dbg = Res("dbg")

    scr = []
    for l in range(L):
        d = {}
        for nm in ("xT0", "xT1", "xT2"):
            d[nm] = dscr(f"{nm}_{l}", [D, S], BF16)
        for nm in ("x1", "x2", "x3"):
            d[nm] = dscr(f"{nm}_{l}", [S, D], F32)
        for nm in ("qaT", "kaT", "qcT", "kcT", "oaT", "ocT"):
            d[nm] = dscr(f"{nm}_{l}", [512, S], BF16)
        d["obT"] = dscr(f"obT_{l}", [D, S], BF16)
        d["mT"] = dscr(f"mT_{l}", [D, S], BF16)
        d["hT"] = dscr(f"hT_{l}", [FFN_H, S], BF16)
        d["va"] = dscr(f"va_{l}", [S, 4 * 130], BF16)
        d["vc"] = dscr(f"vc_{l}", [S, 8 * 66], BF16)
        for p in range(3):
            d[f"Oc{p}"] = dscr(f"Oc{p}_{l}", [S, 8 * 66], F32)
        d["res"] = {k: Res(k) for k in list(d.keys())}
        scr.append(d)
    r_ext = Res("ext")

    with ExitStack() as es:
        P = Prog(nc)
        ARENA_BYTES = 188 * 1024
        arena_t = es.enter_context(nc.sbuf_tensor("arena", [128, ARENA_BYTES], U8))
        PERS = 14 * 1024
        pers = Arena(arena_t, 0, PERS)
        ar = Arena(arena_t, PERS, ARENA_BYTES)
        pb = [es.enter_context(nc.psum_tensor(f"pb{i}", [128, 512], F32)) for i in range(8)]
        rb = [Res(f"pb{i}", excl=True) for i in range(8)]

        def fsz(ap):
            n = 1
            for s_ in ap.shape[1:]:
                n *= s_
            return n

        def mm(out, lhsT, rhs, start, stop, rd, wres, first, sgc=False):
            return P.add("pe", lambda e: e.matmul(out, lhsT, rhs, start=start, stop=stop, skip_group_check=sgc), reads=rd,
                         writes=[wres] if first else (), partial=() if first else [wres],
                         cost=max(64, fsz(rhs)) / 2.4 + 10.0)

        def tr(out, in_, ident, rd, wres, first):
            return P.add("pe", lambda e: e.transpose(out, in_, ident), reads=rd,
                         writes=[wres] if first else (), partial=() if first else [wres], cost=110.0)

        def c_act(ap):
            return fsz(ap) / 1.4 + 220.0

        def c_dve(ap):
            return fsz(ap) / 0.96 * 1.2 + 120.0

        def c_pool(ap):
            return fsz(ap) / 0.9 + 300.0

        def c_eng(eng, ap):
            return {"act": c_act, "dve": c_dve, "pool": c_pool}[eng](ap)

        def act(out, in_, func, rd, wr=(), pw=(), scale=None, bias=None):
            kw = {}
            if scale is not None:
                kw["scale"] = scale
            if bias is not None:
                kw["bias"] = bias
            fam = {AF.Exp: "exp", AF.Ln: "exp", AF.Sigmoid: "sig", AF.Silu: "silu", AF.Sqrt: "sqrt",
                   AF.Square: None}.get(func)
            return P.add("act", lambda e: e.activation(out, in_, func, **kw), reads=rd, writes=wr, partial=pw,
                         cost=c_act(out), tag=fam)

        def tt(eng, out, in0, in1, op, rd, wr=(), pw=()):
            return P.add(eng, lambda e: e.tensor_tensor(out, in0, in1, op), reads=rd, writes=wr, partial=pw,
                         cost=c_eng(eng, out))

        def ts(eng, out, in0, s1, s2, op0, op1, rd, wr=(), pw=()):
            if op1 is None and eng == "pool":
                if op0 == ALU.add:
                    s2, op1 = 1.0, ALU.mult
                elif op0 == ALU.mult:
                    s2, op1 = 0.0, ALU.add
            if op1 is None:
                return P.add(eng, lambda e: e.tensor_scalar(out, in0, s1, None, op0), reads=rd, writes=wr, partial=pw,
                             cost=c_eng(eng, out))
            return P.add(eng, lambda e: e.tensor_scalar(out, in0, s1, s2, op0, op1), reads=rd, writes=wr, partial=pw,
                         cost=c_eng(eng, out))

        def stt(out, in0, scalar, in1, op0, op1, rd, wr=(), pw=()):
            return P.add("dve", lambda e: e.scalar_tensor_tensor(out, in0, scalar, in1, op0, op1), reads=rd,
                         writes=wr, partial=pw, cost=c_dve(out) * 1.5)

        def cp(eng, out, in_, rd, wr=(), pw=()):
            if eng == "act":
                return P.add("act", lambda e: e.copy(out, in_), reads=rd, writes=wr, partial=pw, cost=c_act(out))
            return P.add(eng, lambda e: e.tensor_copy(out, in_), reads=rd, writes=wr, partial=pw, cost=c_eng(eng, out))

        def recip(out, in_, rd, wr=(), pw=()):
            return P.add("dve", lambda e: e.reciprocal(out, in_), reads=rd, writes=wr, partial=pw, cost=c_dve(out))

        def memset(eng, ap, val, wr=(), pw=()):
            return P.add(eng, lambda e: e.memset(ap, val), writes=wr, partial=pw, cost=c_eng(eng, ap))

        def wload(dst, src, wres, last=4096):
            return P.dma("pool", dst, src, reads=[r_ext], partial=[wres], max_dma_last_dim=last)

        ident = pers.alloc([128, 128], F32)
        r_const = Res("const")
        P.dma("sp", ident, c_ident, reads=[r_ext], partial=[r_const])
        rot = pers.alloc([128, NT, 16], F32)
        P.dma("sp", rot, c_rot.rearrange("(t p) c -> p t c", p=128), reads=[r_ext], partial=[r_const])
        maskd = pers.alloc([128, 128], BF16)
        P.dma("pool", maskd, c_maskd, reads=[r_ext], partial=[r_const])
        maskc = pers.alloc([128, 512], BF16)
        P.dma("pool", maskc, c_maskc, reads=[r_ext], partial=[r_const])
        eps_t = pers.alloc([128, 1], F32)
        memset("dve", eps_t, LN_EPS, pw=[r_const])
        one_t = pers.alloc([128, 1], F32)
        memset("dve", one_t, 1.0, pw=[r_const])
        lng = pers.alloc([128, D], F32)
        lnb = pers.alloc([128, D], F32)
        r_lnp = Res("lnp")

        class Fin:
            def __init__(self, do_ln, x_dst, r_xdst, xT_dst, r_xTdst, tb=(6, 7)):
                self.do_ln, self.x_dst, self.r_xdst, self.xT_dst, self.r_xTdst = do_ln, x_dst, r_xdst, xT_dst, r_xTdst
                self.tb = tb
                self.y = [ar.alloc([128, D], F32) for _ in range(3)]
                self.ry = [Res() for _ in range(3)]
                self.yi = 0
                self.xTs = [ar.alloc([128, 8, 512], BF16) for _ in range(2)] if xT_dst is not None else None
                self.rxTs = [Res() for _ in range(2)]
                self.sts = [ar.alloc([128, 2, 16], F32) for _ in range(3)]
                self.rsts = [Res() for _ in range(3)]
                self.cnt = 0

            def tile(self, z, rz, g, t4):
                yb, ryb = self.y[self.yi % 3], self.ry[self.yi % 3]
                self.st, self.rst = self.sts[self.yi % 3], self.rsts[self.yi % 3]
                self.yi += 1
                yt = yb
                if self.do_ln:
                    st = self.st
                    P.add("dve", lambda e: e.bn_stats(st[:, 0, 0:6], z[:, 0:512]), reads=[rz], partial=[self.rst], cost=700.0)
                    P.add("dve", lambda e: e.bn_stats(st[:, 0, 6:12], z[:, 512:1024]), reads=[rz], partial=[self.rst], cost=700.0)
                    P.add("dve", lambda e: e.bn_aggr(st[:, 1, 0:2], st[:, 0, 0:12]), reads=[self.rst], partial=[self.rst])
                    act(st[:, 1, 2:3], st[:, 1, 1:2], AF.Sqrt, [self.rst, r_const], pw=[self.rst], bias=eps_t)
                    recip(st[:, 1, 3:4], st[:, 1, 2:3], [self.rst], pw=[self.rst])
                    ts("dve", yt, z, st[:, 1, 0:1], st[:, 1, 3:4], ALU.subtract, ALU.mult, [rz, self.rst], wr=[ryb])
                    tt("pool", yt, yt, lng, ALU.mult, [ryb, r_lnp], pw=[ryb])
                    tt("pool", yt, yt, lnb, ALU.add, [ryb, r_lnp], pw=[ryb])
                else:
                    cp("pool", yt, z, [rz], pw=[ryb])
                if self.xTs is not None:
                    xs, rxs = self.xTs[g % 2], self.rxTs[g % 2]
                    for h in range(2):
                        b = self.tb[self.cnt % len(self.tb)]
                        self.cnt += 1
                        for j in range(4):
                            kc = h * 4 + j
                            tr(pb[b][:, j * 128:(j + 1) * 128], yt[:, kc * 128:(kc + 1) * 128], ident,
                               [ryb, r_const], rb[b], j == 0)
                        cp("act" if h == 0 else "dve", xs[:, h * 4:(h + 1) * 4, t4 * 128:(t4 + 1) * 128],
                           pb[b][:].rearrange("p (j q) -> p j q", j=4), [rb[b]], pw=[rxs])
                t = g * 4 + t4
                P.dma("pool", self.x_dst[t * 128:(t + 1) * 128, :], yb, reads=[ryb], partial=[self.r_xdst])
                if t4 == 3:
                    if self.xTs is not None:
                        P.dma("pool", self.xT_dst[:, g * 512:(g + 1) * 512].rearrange("(k p) t -> p k t", p=128),
                              self.xTs[g % 2], reads=[self.rxTs[g % 2]], partial=[self.r_xTdst])

        def load_lnp(l, which):
            P.dma("sp", lng, lnp[l, 2 * which:2 * which + 1, :].broadcast_to([128, D]), reads=[r_ext], writes=[r_lnp])
            P.dma("sp", lnb, lnp[l, 2 * which + 1:2 * which + 2, :].broadcast_to([128, D]), reads=[r_ext], partial=[r_lnp])

        final_ops = []

        def phase0(sc):
            ar.reset()
            fin = Fin(False, None, None, sc["xT0"], sc["res"]["xT0"])
            xt = [ar.alloc([128, 4, D], F32) for _ in range(2)]
            rx = [Res() for _ in range(2)]
            for g in range(NG):
                P.dma("sp", xt[g % 2], x_in[g * 512:(g + 1) * 512, :].rearrange("(t p) d -> p t d", p=128),
                      reads=[r_ext], writes=[rx[g % 2]])
                xs, rxs = fin.xTs[g % 2], fin.rxTs[g % 2]
                for t4 in range(4):
                    for h in range(2):
                        b = fin.tb[fin.cnt % 2]
                        fin.cnt += 1
                        for j in range(4):
                            kc = h * 4 + j
                            tr(pb[b][:, j * 128:(j + 1) * 128], xt[g % 2][:, t4, kc * 128:(kc + 1) * 128], ident,
                               [rx[g % 2], r_const], rb[b], j == 0)
                        cp("act" if h == 0 else "dve", xs[:, h * 4:(h + 1) * 4, t4 * 128:(t4 + 1) * 128],
                           pb[b][:].rearrange("p (j q) -> p j q", j=4), [rb[b]], pw=[rxs])
                P.dma("pool", sc["xT0"][:, g * 512:(g + 1) * 512].rearrange("(k p) t -> p k t", p=128), xs,
                      reads=[rxs], partial=[sc["res"]["xT0"]])
            P.barrier()

        def phase1(l, sc):
            ar.reset()
            R = sc["res"]
            colsets = [(0, "q"), (512, "q"), (1024, "va"), (3584, "q"), (4096, "q"), (4608, "vc")]
            qnames = {0: "qaT", 512: "kaT", 3584: "qcT", 4096: "kcT"}
            w = ar.alloc([128, 8, 3072], BF16)
            rw = Res()
            for kc in range(8):
                for ci, (c0, _) in enumerate(colsets):
                    wload(w[:, kc, ci * 512:(ci + 1) * 512], w_in[l, kc * 128:(kc + 1) * 128, c0:c0 + 512], rw)
            xg = [ar.alloc([128, 8, 512], BF16) for _ in range(2)]
            rxg = [Res() for _ in range(2)]
            qk = [ar.alloc([128, 512], F32) for _ in range(2)]
            rqk = [Res() for _ in range(2)]
            tmp = ar.alloc([128, 4, 64], F32)
            rtmp = Res()
            qT = {c0: [ar.alloc([128, 4, 512], BF16) for _ in range(2)] for c0 in qnames}
            rqT = {c0: [Res() for _ in range(2)] for c0 in qnames}
            vas = [ar.alloc([128, 4, 4, 130], BF16) for _ in range(2)]
            rvas = [Res() for _ in range(2)]
            vcs = [ar.alloc([128, 4, 8, 66], BF16) for _ in range(2)]
            rvcs = [Res() for _ in range(2)]
            for i in range(2):
                memset("pool", vas[i][:, :, :, 128:130], 1.0, wr=[rvas[i]])
                memset("pool", vcs[i][:, :, :, 64:66], 1.0, wr=[rvcs[i]])

            def load(g):
                P.dma("sp", xg[g % 2], sc["xT0"][:, g * 512:(g + 1) * 512].rearrange("(k p) t -> p k t", p=128),
                      reads=[R["xT0"]], writes=[rxg[g % 2]])

            load(0)
            bi = 0
            qi = 0
            for g in range(NG):
                if g + 1 < NG:
                    load(g + 1)
                for t4 in range(4):
                    t = g * 4 + t4
                    cosb = rot[:, t, 0:8].unsqueeze(1).to_broadcast([128, 8, 8])
                    sinb = rot[:, t, 8:16].unsqueeze(1).to_broadcast([128, 8, 8])
                    for ci, (c0, kind) in enumerate(colsets):
                        b = bi % 4
                        bi += 1
                        for kc in range(8):
                            mm(pb[b][:], xg[g % 2][:, kc, t4 * 128:(t4 + 1) * 128], w[:, kc, ci * 512:(ci + 1) * 512],
                               kc == 0, kc == 7, [rxg[g % 2], rw], rb[b], kc == 0)
                        if kind == "q":
                            q, rq = qk[qi % 2], rqk[qi % 2]
                            qi += 1
                            ps3 = pb[b][:].rearrange("p (h d) -> p h d", d=64)
                            q3 = q.rearrange("p (h d) -> p h d", d=64)
                            cp("act", q3[:, :, 16:64], ps3[:, :, 16:64], [rb[b]], wr=[rq])
                            x1, x2 = ps3[:, :, 0:8], ps3[:, :, 8:16]
                            tm = tmp.rearrange("p a (h d) -> p a h d", d=8)
                            tt("dve", tm[:, 0], x1, cosb, ALU.mult, [rb[b], r_const], wr=[rtmp])
                            tt("dve", tm[:, 1], x2, sinb, ALU.mult, [rb[b], r_const], pw=[rtmp])
                            tt("dve", tm[:, 2], x2, cosb, ALU.mult, [rb[b], r_const], pw=[rtmp])
                            tt("dve", tm[:, 3], x1, sinb, ALU.mult, [rb[b], r_const], pw=[rtmp])
                            tt("dve", q3[:, :, 0:8], tm[:, 0], tm[:, 1], ALU.subtract, [rtmp], pw=[rq])
                            tt("dve", q3[:, :, 8:16], tm[:, 2], tm[:, 3], ALU.add, [rtmp], pw=[rq])
                            b2 = 4 + (qi % 2)
                            for j in range(4):
                                tr(pb[b2][:, j * 128:(j + 1) * 128], q[:, j * 128:(j + 1) * 128], ident,
                                   [rq, r_const], rb[b2], j == 0)
                            cp("act" if qi % 2 else "dve", qT[c0][g % 2][:, :, t4 * 128:(t4 + 1) * 128],
                               pb[b2][:].rearrange("p (j q) -> p j q", j=4), [rb[b2]], pw=[rqT[c0][g % 2]])
                        elif kind == "va":
                            cp("act", vas[g % 2][:, t4, :, 0:128], pb[b][:].rearrange("p (h d) -> p h d", d=128),
                               [rb[b]], pw=[rvas[g % 2]])
                        else:
                            cp("dve", vcs[g % 2][:, t4, :, 0:64], pb[b][:].rearrange("p (h d) -> p h d", d=64),
                               [rb[b]], pw=[rvcs[g % 2]])
                for c0, nm in qnames.items():
                    P.dma("pool", sc[nm][:, g * 512:(g + 1) * 512].rearrange("(k p) t -> p k t", p=128),
                          qT[c0][g % 2], reads=[rqT[c0][g % 2]], partial=[R[nm]])
                P.dma("pool", sc["va"][g * 512:(g + 1) * 512, :].rearrange("(t p) c -> p t c", p=128),
                      vas[g % 2].rearrange("p t h d -> p t (h d)"), reads=[rvas[g % 2]], partial=[R["va"]])
                P.dma("pool", sc["vc"][g * 512:(g + 1) * 512, :].rearrange("(t p) c -> p t c", p=128),
                      vcs[g % 2].rearrange("p t h d -> p t (h d)"), reads=[rvcs[g % 2]], partial=[R["vc"]])
            P.barrier()

        def phase2(l, sc):
            ar.reset()
            R = sc["res"]
            w = ar.alloc([128, 8, 2048], BF16)
            rw = Res()
            for kc in range(8):
                wload(w[:, kc, :], w_in[l, kc * 128:(kc + 1) * 128, 1536:3584], rw)
            bd = ar.alloc([128, 2, 8, 128], BF16)
            rbd = Res()
            memset("pool", bd, 0.0, wr=[rbd])
            for gt in range(2):
                for c in range(8):
                    for hb in range(2):
                        wload(bd[hb * 64:(hb + 1) * 64, gt, c, hb * 64:(hb + 1) * 64], gate_w[l, gt, 2 * c + hb], rbd)
            prow = ar.alloc([8, D], F32)
            rprow = Res()
            P.dma("sp", prow, rnn_par[l], reads=[r_ext], writes=[rprow])
            par = ar.alloc([128, 8, 16], F32)
            rpar = Res()
            for c in range(8):
                tr(pb[0][:, c * 8:(c + 1) * 8], prow[:, c * 128:(c + 1) * 128], ident[0:8, 0:8], [rprow, r_const],
                   rb[0], c == 0)
            cp("dve", par[:, :, 0:8], pb[0][:, 0:64].rearrange("p (c k) -> p c k", k=8), [rb[0]], wr=[rpar])
            ts("dve", par[:, :, 8:10], par[:, :, 5:7], -1.0, None, ALU.mult, None, [rpar], pw=[rpar])
            act(par[:, :, 12:13], par[:, :, 7:8], AF.Exp, [rpar], pw=[rpar], scale=-1.0)
            ts("dve", par[:, :, 13:14], par[:, :, 12:13], -1.0 / 3.0, 0.5, ALU.mult, ALU.add, [rpar], pw=[rpar])
            tt("dve", par[:, :, 13:14], par[:, :, 13:14], par[:, :, 12:13], ALU.mult, [rpar], pw=[rpar])
            ts("dve", par[:, :, 13:14], par[:, :, 13:14], -1.0, 1.0, ALU.mult, ALU.add, [rpar], pw=[rpar])
            tt("dve", par[:, :, 13:14], par[:, :, 13:14], par[:, :, 12:13], ALU.mult, [rpar], pw=[rpar])
            ts("dve", par[:, :, 10:11], par[:, :, 13:14], -8.0, None, ALU.mult, None, [rpar], pw=[rpar])
            ts("dve", par[:, :, 11:12], par[:, :, 13:14], -16.0, None, ALU.mult, None, [rpar], pw=[rpar])

            xg = [ar.alloc([128, 8, 512], BF16) for _ in range(2)]
            rxg = [Res() for _ in range(2)]
            xin = ar.alloc([128, 8, 520], F32)
            rxin = [Res() for _ in range(8)]
            for c in range(8):
                memset("pool", xin[:, c, 0:3], 0.0, wr=[rxin[c]])
            hprev = ar.alloc([128, 8], F32)
            rhp = [Res() for _ in range(8)]
            memset("pool", hprev, 0.0, wr=rhp)
            NB = 2
            xc = [ar.alloc([128, 512], F32) for _ in range(NB)]
            xcb = [ar.alloc([128, 512], BF16) for _ in range(NB)]
            t_r = [ar.alloc([128, 512], F32) for _ in range(NB)]
            t_i = [ar.alloc([128, 512], F32) for _ in range(NB)]
            t_a = [ar.alloc([128, 512], F32) for _ in range(NB)]
            t_s = [ar.alloc([128, 512], F32) for _ in range(NB)]
            t_h = [ar.alloc([128, 512], F32) for _ in range(NB)]
            t_g = [ar.alloc([128, 512], F32) for _ in range(NB)]
            t_e = [ar.alloc([128, 512], F32) for _ in range(NB)]
            rr = [[Res() for _ in range(NB)] for _ in range(9)]
            obs = [ar.alloc([128, 8, 512], BF16) for _ in range(2)]
            robs = [Res() for _ in range(2)]
            GC = 2.0 * math.sqrt(2.0 / math.pi)

            def load(g):
                P.dma("sp", xg[g % 2], sc["xT0"][:, g * 512:(g + 1) * 512].rearrange("(k p) t -> p k t", p=128),
                      reads=[R["xT0"]], writes=[rxg[g % 2]])

            load(0)
            it = 0
            for g in range(NG):
                if g + 1 < NG:
                    load(g + 1)
                for c in range(8):
                    k = it % NB
                    it += 1
                    bA, bB, bC, bD = [(it % 2) * 4 + j for j in range(4)]
                    for kc in range(8):
                        mm(pb[bA][:], w[:, kc, c * 128:(c + 1) * 128], xg[g % 2][:, kc, :], kc == 0, kc == 7,
                           [rw, rxg[g % 2]], rb[bA], kc == 0)
                    for kc in range(8):
                        mm(pb[bB][:], w[:, kc, 1024 + c * 128:1024 + (c + 1) * 128], xg[g % 2][:, kc, :], kc == 0,
                           kc == 7, [rw, rxg[g % 2]], rb[bB], kc == 0)
                    xi = xin[:, c, :]
                    cp("act", xi[:, 3:515], pb[bA][:], [rb[bA]], pw=[rxin[c]])
                    R_xc, R_xcb, R_r, R_i, R_a, R_s, R_h, R_g, R_e = [rr[j][k] for j in range(9)]
                    ts("dve", xc[k], xi[:, 0:512], par[:, c, 0:1], par[:, c, 4:5], ALU.mult, ALU.add,
                       [rxin[c], rpar], wr=[R_xc])
                    for tap in range(1, 4):
                        stt(xc[k], xi[:, tap:tap + 512], par[:, c, tap:tap + 1], xc[k], ALU.mult, ALU.add,
                            [rxin[c], rpar, R_xc], pw=[R_xc])
                    cp("pool", xi[:, 0:3], xi[:, 512:515], [rxin[c]], pw=[rxin[c]])
                    cp("act", xcb[k], xc[k], [R_xc], wr=[R_xcb])
                    mm(pb[bC][:], bd[:, 0, c, :], xcb[k], True, True, [rbd, R_xcb], rb[bC], True)
                    mm(pb[bD][:], bd[:, 1, c, :], xcb[k], True, True, [rbd, R_xcb], rb[bD], True)
                    cp("act", t_g[k], pb[bB][:], [rb[bB]], wr=[R_g])
                    tt("pool", t_e[k], t_g[k], t_g[k], ALU.mult, [R_g], wr=[R_e])
                    ts("pool", t_e[k], t_e[k], 0.044715, 1.0, ALU.mult, ALU.add, [R_e], pw=[R_e])
                    tt("pool", t_e[k], t_e[k], t_g[k], ALU.mult, [R_e, R_g], pw=[R_e])
                    act(t_r[k], pb[bC][:], AF.Sigmoid, [rb[bC], rpar], wr=[R_r], bias=par[:, c, 5:6])
                    act(t_i[k], pb[bD][:], AF.Sigmoid, [rb[bD], rpar], wr=[R_i], bias=par[:, c, 6:7])
                    act(t_e[k], t_e[k], AF.Sigmoid, [R_e], pw=[R_e], scale=GC)
                    act(t_a[k], t_r[k], AF.Exp, [R_r, rpar], wr=[R_a], scale=par[:, c, 10:11])
                    act(t_s[k], t_r[k], AF.Exp, [R_r, rpar], wr=[R_s], scale=par[:, c, 11:12])
                    act(t_s[k], t_s[k], AF.Ln, [R_s, r_const], pw=[R_s], scale=-1.0, bias=one_t)
                    act(t_s[k], t_s[k], AF.Exp, [R_s], pw=[R_s], scale=0.5)
                    tt("pool", t_i[k], t_i[k], xc[k], ALU.mult, [R_i, R_xc], pw=[R_i])
                    tt("dve", t_s[k], t_s[k], t_i[k], ALU.mult, [R_s, R_i], pw=[R_s])
                    hp = hprev[:, c:c + 1]
                    P.add("dve", lambda e, o=t_h[k], a=t_a[k], u=t_s[k], hp=hp: e.tensor_tensor_scan(
                        o, a, u, hp, ALU.mult, ALU.add), reads=[R_a, R_s, rhp[c]], writes=[R_h], cost=1300.0)
                    cp("dve", hp, t_h[k][:, 511:512], [R_h], wr=[rhp[c]])
                    tt("pool", t_g[k], t_g[k], t_e[k], ALU.mult, [R_g, R_e], pw=[R_g])
                    if dbgbuf is not None and g == 0 and c == 0:
                        for di, (tl, rl) in enumerate(((xc[k], R_xc), (t_r[k], R_r), (t_i[k], R_i), (t_a[k], R_a),
                                                       (t_s[k], R_s), (t_h[k], R_h), (t_g[k], R_g), (t_e[k], R_e))):
                            P.dma("sp", dbgbuf[di], tl, reads=[rl], partial=[r_dbg])
                        P.dma("sp", dbgbuf[8, :, 0:128], par.rearrange("p c k -> p (c k)"), reads=[rpar], partial=[r_dbg])
                    tt("dve", obs[g % 2][:, c, :], t_g[k], t_h[k], ALU.mult, [R_g, R_h], pw=[robs[g % 2]])
                P.dma("pool", sc["obT"][:, g * 512:(g + 1) * 512].rearrange("(k p) t -> p k t", p=128), obs[g % 2],
                      reads=[robs[g % 2]], partial=[R["obT"]])
            P.barrier()

        def phase3(l, sc):
            ar.reset()
            R = sc["res"]
            lambda_init = 0.8 - 0.6 * math.exp(-0.3 * l)
            lq = ar.alloc([128, 256], F32)
            rlq = Res()
            P.dma("sp", lq, lam_qk[l:l + 1, :].broadcast_to([128, 256]), reads=[r_ext], writes=[rlq])
            lt = ar.alloc([128, 8], F32)
            prod = ar.alloc([128, 128], F32)
            tt("dve", prod[:, 0:64], lq[:, 0:64], lq[:, 64:128], ALU.mult, [rlq], pw=[rlq])
            tt("dve", prod[:, 64:128], lq[:, 128:192], lq[:, 192:256], ALU.mult, [rlq], pw=[rlq])
            P.add("dve", lambda e: e.tensor_reduce(lt[:, 0:2], prod.rearrange("p (a b) -> p a b", a=2),
                                                   mybir.AxisListType.X, ALU.add), reads=[rlq], partial=[rlq])
            act(lt[:, 2:4], lt[:, 0:2], AF.Exp, [rlq], pw=[rlq])
            tt("dve", lt[:, 4:5], lt[:, 2:3], lt[:, 3:4], ALU.subtract, [rlq], pw=[rlq])
            ts("dve", lt[:, 5:6], lt[:, 4:5], lambda_init, -1.0, ALU.add, ALU.mult, [rlq], pw=[rlq])
            nlam = lt[:, 5:6]
            gsub = ar.alloc([128, 128], F32)
            P.dma("sp", gsub, diff_subln[l:l + 1, :].broadcast_to([128, 128]), reads=[r_ext], partial=[rlq])
            ts("dve", gsub, gsub, 1.0 - lambda_init, None, ALU.mult, None, [rlq], pw=[rlq])

            gcol = ar.alloc([128, 1], F32)
            P.dma("sp", gcol, diff_subln[l:l + 1, :].rearrange("a d -> d a"), reads=[r_ext], partial=[rlq])
            ts("dve", gcol, gcol, 1.0 - lambda_init, None, ALU.mult, None, [rlq], pw=[rlq])
            ones = ar.alloc([128, 128], BF16)
            memset("dve", ones, 1.0, pw=[rlq])
            kT = [[ar.alloc([128, S], BF16) for _ in range(2)] for _ in range(2)]
            qT = [ar.alloc([128, S], BF16) for _ in range(2)]
            V = [ar.alloc([128, NT, 128], BF16) for _ in range(2)]
            rh = [Res() for _ in range(2)]
            for i_ in range(2):
                memset("pool", kT[i_][0][64:128, :], 0.0, pw=[rh[i_]])
                memset("pool", kT[i_][1][0:64, :], 0.0, pw=[rh[i_]])
            NPB = 3
            pT = [[ar.alloc([128, 512], BF16) for _ in range(NPB)] for _ in range(2)]
            rpT = [[Res() for _ in range(NPB)] for _ in range(2)]
            NE = 2
            L1 = [ar.alloc([128, 512], F32) for _ in range(NE)]
            L2 = [ar.alloc([128, 512], F32) for _ in range(NE)]
            oA = [ar.alloc([128, 512], F32) for _ in range(NE)]
            oB = [ar.alloc([128, 512], F32) for _ in range(NE)]
            sqb = [ar.alloc([128, 512], BF16) for _ in range(NE)]
            rsd = [ar.alloc([128, 512], F32) for _ in range(NE)]
            rE = [[Res() for _ in range(NE)] for _ in range(6)]
            oaS = [ar.alloc([128, 512], BF16) for _ in range(2)]
            roaS = [Res() for _ in range(2)]

            def load(h):
                P.dma("sp", kT[h % 2][0][0:64, :], sc["kaT"][h * 128:h * 128 + 64, :], reads=[R["kaT"]],
                      partial=[rh[h % 2]])
                P.dma("sp", kT[h % 2][1][64:128, :], sc["kaT"][h * 128 + 64:(h + 1) * 128, :], reads=[R["kaT"]],
                      partial=[rh[h % 2]])
                P.dma("sp", qT[h % 2], sc["qaT"][h * 128:(h + 1) * 128, :], reads=[R["qaT"]], partial=[rh[h % 2]])
                P.dma("sp", V[h % 2], sc["va"].rearrange("(t p) (h d) -> p t h d", p=128, d=130)[:, :, h, 0:128],
                      reads=[R["va"]], partial=[rh[h % 2]])

            load(0)
            sbi = 0
            pbi = 0
            gi = 0
            OA, LS, RB = (3, 4), (5, 6), 7
            for h in range(4):
                if h + 1 < 4:
                    load(h + 1)
                kt, qt, vt, rhh = kT[h % 2], qT[h % 2], V[h % 2], rh[h % 2]
                for g in range(NG):
                    e2 = gi % NE
                    gi += 1
                    nkb = 4 * g + 4
                    for j in range(nkb):
                        r = j - 4 * g
                        q0 = max(r, 0)
                        s3 = pbi % NPB
                        pbi += 1
                        cs = slice(q0 * 128, 512)
                        for c in range(2):
                            b = sbi % 3
                            sbi += 1
                            mm(pb[b][:, cs], kt[c][:, j * 128:(j + 1) * 128],
                               qt[:, g * 512 + q0 * 128:(g + 1) * 512], True, True, [rhh], rb[b], True)
                            act(pT[c][s3][:, cs], pb[b][:, cs], AF.Exp, [rb[b]], wr=[rpT[c][s3]], scale=0.125)
                            if r >= 0:
                                tt("dve", pT[c][s3][:, r * 128:(r + 1) * 128], pT[c][s3][:, r * 128:(r + 1) * 128],
                                   maskd, ALU.mult, [rpT[c][s3], r_const], pw=[rpT[c][s3]])
                        for c in range(2):
                            mm(pb[OA[c]][:, cs], vt[:, j, :], pT[c][s3][:, cs], j == 0, j == nkb - 1,
                               [rpT[c][s3], rhh], rb[OA[c]], j == 0)
                            mm(pb[LS[c]][:, cs], ones, pT[c][s3][:, cs], j == 0, j == nkb - 1,
                               [rpT[c][s3], rlq], rb[LS[c]], j == 0)
                    r1, r2, ra, rb_, rq, rr_ = [rE[k_][e2] for k_ in range(6)]
                    recip(L1[e2], pb[LS[0]][:], [rb[LS[0]]], wr=[r1])
                    recip(L2[e2], pb[LS[1]][:], [rb[LS[1]]], wr=[r2])
                    tt("dve", oA[e2], pb[OA[0]][:], L1[e2], ALU.mult, [rb[OA[0]], r1], wr=[ra])
                    stt(oB[e2], pb[OA[1]][:], nlam, L2[e2], ALU.mult, ALU.mult, [rb[OA[1]], r2, rlq], wr=[rb_])
                    tt("pool", oA[e2], oA[e2], oB[e2], ALU.add, [ra, rb_], pw=[ra])
                    act(sqb[e2], oA[e2], AF.Square, [ra], wr=[rq])
                    mm(pb[RB][:], ones, sqb[e2], True, True, [rq, rlq], rb[RB], True)
                    act(rsd[e2], pb[RB][:], AF.Sqrt, [rb[RB], r_const], wr=[rr_], scale=1.0 / 128.0, bias=eps_t)
                    recip(rsd[e2], rsd[e2], [rr_], pw=[rr_])
                    o2 = gi % 2
                    stt(oaS[o2], oA[e2], gcol, rsd[e2], ALU.mult, ALU.mult, [ra, rr_, rlq], wr=[roaS[o2]])
                    P.dma("pool", sc["oaT"][h * 128:(h + 1) * 128, g * 512:(g + 1) * 512], oaS[o2],
                          reads=[roaS[o2]], partial=[R["oaT"]])
            P.barrier()

        def phase4(l, sc):
            ar.reset()
            R = sc["res"]
            kT = [[ar.alloc([128, S], BF16) for _ in range(2)] for _ in range(2)]
            qT = [ar.alloc([128, S], BF16) for _ in range(2)]
            rkq = [Res() for _ in range(2)]
            for i_ in range(2):
                memset("pool", kT[i_][0][64:128, :], 0.0, pw=[rkq[i_]])
                memset("pool", kT[i_][1][0:64, :], 0.0, pw=[rkq[i_]])
            V = [ar.alloc([128, NT, 2, 66], BF16) for _ in range(2)]
            rV = [Res() for _ in range(2)]
            pT = [ar.alloc([128, 512], BF16) for _ in range(2)]
            rpT = [Res() for _ in range(2)]
            OCH = 16
            Ost = [ar.alloc([128, OCH, 132], F32) for _ in range(2)]
            rOst = [Res() for _ in range(2)]
            pats = [(1, 0), (4, 1), (16, 2)]
            import os
            if os.environ.get("P4PATS"):
                pats = [pats[int(c)] for c in os.environ["P4PATS"].split(",")]

            def loadkq(hp):
                P.dma("sp", kT[hp % 2][0][0:64, :], sc["kcT"][hp * 128:hp * 128 + 64, :], reads=[R["kcT"]],
                      partial=[rkq[hp % 2]])
                P.dma("sp", kT[hp % 2][1][64:128, :], sc["kcT"][hp * 128 + 64:(hp + 1) * 128, :], reads=[R["kcT"]],
                      partial=[rkq[hp % 2]])
                P.dma("sp", qT[hp % 2], sc["qcT"][hp * 128:(hp + 1) * 128, :], reads=[R["qcT"]], partial=[rkq[hp % 2]])

            vi = 0
            oi = 0
            si = 0
            loadkq(0)
            for hp in range(4):
                if hp + 1 < 4:
                    loadkq(hp + 1)
                kt, qt, rk = kT[hp % 2], qT[hp % 2], rkq[hp % 2]
                for (dl, pi) in pats:
                    nbp = S // (128 * dl)
                    v, rv = V[vi % 2], rV[vi % 2]
                    vi += 1
                    vsrc = sc["vc"].rearrange("(n i p) (h d) -> i p n h d", i=128, p=dl, d=66)
                    for p in range(dl):
                        P.dma("sp", v[:, p * nbp:(p + 1) * nbp, :, :], vsrc[:, p, :, 2 * hp:2 * hp + 2, :],
                              reads=[R["vc"]], writes=[rv] if p == 0 else (), partial=() if p == 0 else [rv])
                    odst = sc[f"Oc{pi}"].rearrange("(n i p) c -> i p n c", i=128, p=dl)
                    for p in range(dl):
                        for n in range(nbp):
                            if n % OCH == 0:
                                ost, ro = Ost[oi % 2], rOst[oi % 2]
                                oi += 1
                            s2 = si % 2
                            si += 1
                            bO = 4 + s2
                            qc0 = p + dl * n * 128
                            qsl = slice(qc0, qc0 + 127 * dl + 1, dl)
                            for hh in range(2):
                                bS = 2 * s2 + hh
                                first = True
                                for pc in range(2):
                                    if pc == 0 and n == 0:
                                        continue
                                    kn = n - 1 + pc
                                    kc0 = p + dl * kn * 128
                                    ksl = slice(kc0, kc0 + 127 * dl + 1, dl)
                                    mm(pb[bS][:, pc * 128:(pc + 1) * 128], kt[hh][:, ksl],
                                       qt[:, qsl], True, True, [rk], rb[bS], first)
                                    first = False
                                lo = 128 if n == 0 else 0
                                act(pT[s2][:, hh * 256 + lo:(hh + 1) * 256], pb[bS][:, lo:256], AF.Exp, [rb[bS]],
                                    wr=[rpT[s2]] if hh == 0 else (), pw=() if hh == 0 else [rpT[s2]], scale=0.125)
                            if n == 0:
                                dst = pT[s2].rearrange("p (h c q) -> p h c q", h=2, c=2)[:, :, 1, :]
                                msk = maskc.rearrange("p (h c q) -> p h c q", h=2, c=2)[:, :, 1, :]
                            else:
                                dst, msk = pT[s2], maskc
                            tt("dve", dst, dst, msk, ALU.mult, [rpT[s2], r_const], pw=[rpT[s2]])
                            first = True
                            for hh in range(2):
                                pcs = [1] if n == 0 else [0, 1]
                                for pc in pcs:
                                    kn = n - 1 + pc
                                    col = (hh * 2 + pc) * 128
                                    mm(pb[bO][:, hh * 66:hh * 66 + 65], pT[s2][:, col:col + 128],
                                       v[:, p * nbp + kn, hh, 0:65], pc == pcs[0], pc == 1, [rpT[s2], rv], rb[bO], first)
                                    first = False
                            no = n % OCH
                            osl = ost[:, no, :].rearrange("p (h d) -> p h d", h=2)[:, :, 0:65]
                            isl = pb[bO][:, 0:132].rearrange("p (h d) -> p h d", h=2)[:, :, 0:65]
                            cp("dve" if n % 2 else "act", osl, isl, [rb[bO]], wr=[ro] if no == 0 else (),
                               pw=() if no == 0 else [ro])
                            if no == OCH - 1 or n == nbp - 1:
                                n0 = n - no
                                P.dma("pool", odst[:, p, n0:n + 1, hp * 132:(hp + 1) * 132], ost[:, 0:no + 1, :],
                                      reads=[ro], partial=[R[f"Oc{pi}"]])
            P.barrier()
            if os.environ.get("P4NOCOMB"):
                return
            ar.reset()
            oin = [[ar.alloc([128, 4, 8, 66], F32) for _ in range(3)] for _ in range(2)]
            roin = [Res() for _ in range(2)]
            rcp = ar.alloc([128, 4, 8], F32)
            oc = [ar.alloc([128, 4, 512], F32) for _ in range(2)]
            roc = [Res() for _ in range(2)]
            ocS = [ar.alloc([128, 4, 512], BF16) for _ in range(2)]
            rocS = [Res() for _ in range(2)]

            def loadO(g):
                for pi in range(3):
                    P.dma("sp", oin[g % 2][pi].rearrange("p t h d -> p t (h d)"),
                          sc[f"Oc{pi}"][g * 512:(g + 1) * 512, :].rearrange("(t p) c -> p t c", p=128),
                          reads=[R[f"Oc{pi}"]], writes=[roin[g % 2]] if pi == 0 else (),
                          partial=() if pi == 0 else [roin[g % 2]])

            loadO(0)
            ti = 0
            for g in range(NG):
                if g + 1 < NG:
                    loadO(g + 1)
                a, b_, c_ = oin[g % 2]
                ro = roin[g % 2]
                af, bf, cf = [t_.rearrange("p t h d -> p (t h d)") for t_ in (a, b_, c_)]
                tt("pool", af, af, bf, ALU.add, [ro], pw=[ro])
                tt("dve", af, af, cf, ALU.add, [ro], pw=[ro])
                recip(rcp, a[:, :, :, 64], [ro], wr=[roc[g % 2]])
                for t4 in range(4):
                    tt("dve", oc[g % 2][:, t4, :].rearrange("p (h d) -> p h d", d=64), a[:, t4, :, 0:64],
                       rcp[:, t4, :].unsqueeze(2).to_broadcast([128, 8, 64]), ALU.mult, [ro, roc[g % 2]],
                       pw=[roc[g % 2]])
                    b = 4 + ti % 2
                    ti += 1
                    for j in range(4):
                        tr(pb[b][:, j * 128:(j + 1) * 128], oc[g % 2][:, t4, j * 128:(j + 1) * 128], ident,
                           [roc[g % 2], r_const], rb[b], j == 0)
                    cp("act", ocS[g % 2][:, :, t4 * 128:(t4 + 1) * 128], pb[b][:].rearrange("p (j q) -> p j q", j=4),
                       [rb[b]], pw=[rocS[g % 2]])
                P.dma("pool", sc["ocT"][:, g * 512:(g + 1) * 512].rearrange("(k p) t -> p k t", p=128), ocS[g % 2],
                      reads=[rocS[g % 2]], partial=[R["ocT"]])
            P.barrier()

        def outproj_ln(fin, g, t4, lhs_chunks, wmat, nk, rd, xt, rxt, z, rz, ob):
            for half in range(2):
                b = ob[half]
                for k in range(nk):
                    mm(pb[b][:], lhs_chunks(k), wmat[:, k, half * 512:(half + 1) * 512], k == 0, k == nk - 1, rd,
                       rb[b], k == 0)
                stt(z[:, half * 512:(half + 1) * 512], xt[:, half * 512:(half + 1) * 512], ALPHA, pb[b][:],
                    ALU.mult, ALU.add, [rxt, rb[b]], wr=[rz] if half == 0 else (), pw=() if half == 0 else [rz])
            fin.tile(z, rz, g, t4)

        class XT:
            def __init__(self, x_src, r_xsrc):
                self.x_src, self.r_xsrc = x_src, r_xsrc
                self.buf = [ar.alloc([128, D], F32) for _ in range(3)]
                self.res = [Res() for _ in range(3)]
                self.issued = -1

            def _load(self, t):
                if t >= NT or t <= self.issued:
                    return
                self.issued = t
                P.dma("sp", self.buf[t % 3], self.x_src[t * 128:(t + 1) * 128, :], reads=[self.r_xsrc],
                      writes=[self.res[t % 3]])

            def get(self, t):
                self._load(t)
                self._load(t + 1)
                return self.buf[t % 3], self.res[t % 3]

        def phase5a(l, sc):
            ar.reset()
            R = sc["res"]
            wg = ar.alloc([128, 8, 3072], BF16)
            wa = ar.alloc([128, 4, D], BF16)
            wb = ar.alloc([128, 8, D], BF16)
            wc = ar.alloc([128, 4, D], BF16)
            rw = Res()
            for kc in range(8):
                wload(wg[:, kc, :], w_in[l, kc * 128:(kc + 1) * 128, 5120:8192], rw)
                wload(wb[:, kc, :], w_br_b[l, kc * 128:(kc + 1) * 128, :], rw)
            for kc in range(4):
                wload(wa[:, kc, :], w_br_a[l, kc * 128:(kc + 1) * 128, :], rw)
                wload(wc[:, kc, :], w_br_c[l, kc * 128:(kc + 1) * 128, :], rw)
            xg = [ar.alloc([128, 8, 512], BF16) for _ in range(2)]
            oa = [ar.alloc([128, 4, 512], BF16) for _ in range(2)]
            ob_ = [ar.alloc([128, 8, 512], BF16) for _ in range(2)]
            oc = [ar.alloc([128, 4, 512], BF16) for _ in range(2)]
            rin = [Res() for _ in range(2)]
            sg = [ar.alloc([128, 512], F32) for _ in range(6)]
            rsg = [Res() for _ in range(6)]
            macc = [ar.alloc([128, 512], F32) for _ in range(2)]
            rmacc = [Res() for _ in range(2)]
            mT = [ar.alloc([128, 8, 512], BF16) for _ in range(2)]
            rmT = [Res() for _ in range(2)]

            def load(g):
                sl = slice(g * 512, (g + 1) * 512)
                k = g % 2
                P.dma("sp", xg[k], sc["xT0"][:, sl].rearrange("(k p) t -> p k t", p=128), reads=[R["xT0"]], writes=[rin[k]])
                P.dma("sp", oa[k], sc["oaT"][:, sl].rearrange("(k p) t -> p k t", p=128), reads=[R["oaT"]], partial=[rin[k]])
                P.dma("sp", ob_[k], sc["obT"][:, sl].rearrange("(k p) t -> p k t", p=128), reads=[R["obT"]], partial=[rin[k]])
                P.dma("sp", oc[k], sc["ocT"][:, sl].rearrange("(k p) t -> p k t", p=128), reads=[R["ocT"]], partial=[rin[k]])

            load(0)
            it = 0
            for g in range(NG):
                if g + 1 < NG:
                    load(g + 1)
                k = g % 2
                for dc in range(8):
                    dsl = slice(dc * 128, (dc + 1) * 128)
                    i2 = it % 2
                    it += 1
                    bg = [0, 1, 2] if i2 == 0 else [3, 4, 5]
                    s3 = sg[3 * i2:3 * i2 + 3]
                    rs3 = rsg[3 * i2:3 * i2 + 3]
                    for gi in range(3):
                        for kc in range(8):
                            mm(pb[bg[gi]][:], wg[:, kc, gi * 1024 + dc * 128:gi * 1024 + (dc + 1) * 128], xg[k][:, kc, :],
                               kc == 0, kc == 7, [rw, rin[k]], rb[bg[gi]], kc == 0)
                        act(s3[gi], pb[bg[gi]][:], AF.Sigmoid, [rb[bg[gi]]], wr=[rs3[gi]])
                    ma, rma = macc[i2], rmacc[i2]
                    for bi_, (wm, src_, nk) in enumerate(((wa, oa[k], 4), (wb, ob_[k], 8), (wc, oc[k], 4))):
                        b = 6 + (bi_ % 2)
                        for kc in range(nk):
                            mm(pb[b][:], wm[:, kc, dsl], src_[:, kc, :], kc == 0, kc == nk - 1, [rw, rin[k]], rb[b], kc == 0)
                        if bi_ == 0:
                            tt("dve", ma, pb[b][:], s3[0], ALU.mult, [rb[b], rs3[0]], wr=[rma])
                        else:
                            tt("dve", s3[bi_], pb[b][:], s3[bi_], ALU.mult, [rb[b], rs3[bi_]], pw=[rs3[bi_]])
                    tt("pool", ma, ma, s3[1], ALU.add, [rma, rs3[1]], pw=[rma])
                    tt("dve", mT[k][:, dc, :], ma, s3[2], ALU.add, [rma, rs3[2]], wr=[rmT[k]] if dc == 0 else (),
                       pw=() if dc == 0 else [rmT[k]])
                P.dma("pool", sc["mT"][:, g * 512:(g + 1) * 512].rearrange("(k p) t -> p k t", p=128), mT[k],
                      reads=[rmT[k]], partial=[R["mT"]])
            P.barrier()

        def phase_out(l, sc, srcname, nk, wdram, lnidx, x_src, r_xsrc, x_dst, r_xdst, xT_dst, r_xTdst):
            ar.reset()
            R = sc["res"]
            load_lnp(l, lnidx)
            wm = ar.alloc([128, nk, D], BF16)
            rw = Res()
            for kc in range(nk):
                wload(wm[:, kc, :], wdram[l, kc * 128:(kc + 1) * 128, :], rw)
            fin = Fin(True, x_dst, r_xdst, xT_dst, r_xTdst, tb=(6, 7))
            sg_ = [ar.alloc([128, nk, 512], BF16) for _ in range(2)]
            rsg_ = [Res() for _ in range(2)]
            xtl = XT(x_src, r_xsrc)
            z = [ar.alloc([128, D], F32) for _ in range(3)]
            rz = [Res() for _ in range(3)]

            def load(g):
                k = g % 2
                P.dma("sp", sg_[k], sc[srcname][:, g * 512:(g + 1) * 512].rearrange("(k p) t -> p k t", p=128),
                      reads=[R[srcname]], writes=[rsg_[k]])

            load(0)
            zi = 0
            for g in range(NG):
                if g + 1 < NG:
                    load(g + 1)
                k = g % 2
                for t4 in range(4):
                    xt, rxt = xtl.get(g * 4 + t4)
                    zz, rzz = z[zi % 3], rz[zi % 3]
                    ob = ((0, 1), (2, 3), (4, 5))[zi % 3]
                    zi += 1
                    outproj_ln(fin, g, t4, lambda kk, t4=t4, k=k: sg_[k][:, kk, t4 * 128:(t4 + 1) * 128], wm, nk,
                               [rsg_[k], rw], xt, rxt, zz, rzz, ob)
            P.barrier()

        def phase6(l, sc):
            ar.reset()
            R = sc["res"]
            load_lnp(l, 1)
            wq = ar.alloc([128, 8, D], BF16)
            wo = ar.alloc([128, 8, D], BF16)
            rw = Res()
            KT = ar.alloc([128, 8, MEM_T], BF16)
            Vx = ar.alloc([128, 2, D], BF16)
            ones = ar.alloc([128, 128], BF16)
            rkv = Res()
            memset("dve", ones, 1.0, wr=[rkv])
            mark = ar.off
            wkv = ar.alloc([128, 8, 2 * D], BF16)
            rwkv = Res()
            for kc in range(8):
                wload(wkv[:, kc, :], xkv[l, kc * 128:(kc + 1) * 128, :], rwkv)
                wload(wq[:, kc, :], xq[l, kc * 128:(kc + 1) * 128, :], rw)
                wload(wo[:, kc, :], xo[l, kc * 128:(kc + 1) * 128, :], rw)
            mt = ar.alloc([128, 2, D], F32)
            rmt = Res()
            P.dma("sp", mt, mem.rearrange("(t p) d -> p t d", p=128), reads=[r_ext], writes=[rmt])
            memT = ar.alloc([128, 8, MEM_T], BF16)
            rmemT = Res()
            bi = 0
            for m2 in range(2):
                for h in range(2):
                    b = bi % 4
                    bi += 1
                    for j in range(4):
                        kc = h * 4 + j
                        tr(pb[b][:, j * 128:(j + 1) * 128], mt[:, m2, kc * 128:(kc + 1) * 128], ident, [rmt, r_const],
                           rb[b], j == 0)
                    cp("dve", memT[:, h * 4:(h + 1) * 4, m2 * 128:(m2 + 1) * 128],
                       pb[b][:].rearrange("p (j q) -> p j q", j=4), [rb[b]], pw=[rmemT])
            for fc in range(8):
                b = bi % 4
                bi += 1
                for kc in range(8):
                    mm(pb[b][:, 0:MEM_T], wkv[:, kc, fc * 128:(fc + 1) * 128], memT[:, kc, :], kc == 0, kc == 7,
                       [rwkv, rmemT], rb[b], kc == 0)
                cp("act", KT[:, fc, :], pb[b][:, 0:MEM_T], [rb[b]], pw=[rkv])
            for m2 in range(2):
                for half in range(2):
                    b = bi % 4
                    bi += 1
                    for kc in range(8):
                        mm(pb[b][:], memT[:, kc, m2 * 128:(m2 + 1) * 128], wkv[:, kc, D + half * 512:D + (half + 1) * 512],
                           kc == 0, kc == 7, [rwkv, rmemT], rb[b], kc == 0)
                    cp("dve", Vx[:, m2, half * 512:(half + 1) * 512], pb[b][:], [rb[b]], pw=[rkv])
            P.barrier()
            ar.off = mark
            fin = Fin(True, sc["x2"], R["x2"], sc["xT2"], R["xT2"], tb=(6, 7))
            xg = [ar.alloc([128, 8, 512], BF16) for _ in range(2)]
            rxg = [Res() for _ in range(2)]
            xtl = XT(sc["x1"], R["x1"])
            qT = ar.alloc([128, 8, 512], BF16)
            rqT = Res()
            pT = [ar.alloc([128, 2, 512], BF16) for _ in range(2)]
            rpT = [Res() for _ in range(2)]
            rs = [ar.alloc([128, 512], F32) for _ in range(2)]
            rrs = [Res() for _ in range(2)]
            oT = ar.alloc([128, 8, 512], BF16)
            roT = Res()
            z = [ar.alloc([128, D], F32) for _ in range(2)]
            rz = [Res() for _ in range(2)]

            def load(g):
                sl = slice(g * 512, (g + 1) * 512)
                k = g % 2
                P.dma("sp", xg[k], sc["xT1"][:, sl].rearrange("(k p) t -> p k t", p=128), reads=[R["xT1"]], writes=[rxg[k]])

            load(0)
            zi = 0
            hi = 0
            for g in range(NG):
                if g + 1 < NG:
                    load(g + 1)
                k = g % 2
                for fc in range(8):
                    b = fc % 2
                    for kc in range(8):
                        mm(pb[b][:], wq[:, kc, fc * 128:(fc + 1) * 128], xg[k][:, kc, :], kc == 0, kc == 7,
                           [rw, rxg[k]], rb[b], kc == 0)
                    cp("act" if fc % 2 else "dve", qT[:, fc, :], pb[b][:], [rb[b]], wr=[rqT] if fc == 0 else (),
                       pw=() if fc == 0 else [rqT])
                for h in range(4):
                    h2 = hi % 2
                    hi += 1
                    for m2 in range(2):
                        b = 2 + m2
                        for c2 in range(2):
                            mm(pb[b][:], KT[:, 2 * h + c2, m2 * 128:(m2 + 1) * 128], qT[:, 2 * h + c2, :], c2 == 0,
                               c2 == 1, [rkv, rqT], rb[b], c2 == 0)
                        act(pT[h2][:, m2, :], pb[b][:], AF.Exp, [rb[b]], wr=[rpT[h2]] if m2 == 0 else (),
                            pw=() if m2 == 0 else [rpT[h2]], scale=1.0 / 16.0)
                    for m2 in range(2):
                        mm(pb[4][:], ones, pT[h2][:, m2, :], m2 == 0, m2 == 1, [rkv, rpT[h2]], rb[4], m2 == 0)
                    recip(rs[h2], pb[4][:], [rb[4]], wr=[rrs[h2]])
                    for e2 in range(2):
                        b = 5 if e2 == 0 else (0 if h % 2 == 0 else 1)
                        for m2 in range(2):
                            mm(pb[b][:], Vx[:, m2, h * 256 + e2 * 128:h * 256 + (e2 + 1) * 128], pT[h2][:, m2, :],
                               m2 == 0, m2 == 1, [rkv, rpT[h2]], rb[b], m2 == 0)
                        first = (h == 0 and e2 == 0)
                        tt("dve", oT[:, 2 * h + e2, :], pb[b][:], rs[h2], ALU.mult, [rb[b], rrs[h2]],
                           wr=[roT] if first else (), pw=() if first else [roT])
                for t4 in range(4):
                    xt, rxt = xtl.get(g * 4 + t4)
                    zz, rzz = z[zi % 2], rz[zi % 2]
                    zi += 1
                    outproj_ln(fin, g, t4, lambda kk, t4=t4: oT[:, kk, t4 * 128:(t4 + 1) * 128], wo, 8, [roT, rw],
                               xt, rxt, zz, rzz, (2, 3) if t4 % 2 == 0 else (4, 5))
            P.barrier()

        def phase7a(l, sc):
            ar.reset()
            R = sc["res"]
            NH = FFN_H // 128
            wu = ar.alloc([128, 8, 2 * FFN_H], BF16)
            rw = Res()
            for kc in range(8):
                wload(wu[:, kc, 0:FFN_H], w_up[l, kc * 128:(kc + 1) * 128, 0:FFN_H], rw)
                wload(wu[:, kc, FFN_H:2 * FFN_H], w_up[l, kc * 128:(kc + 1) * 128, FFN_H:2 * FFN_H], rw)
            xg = [ar.alloc([128, 8, 512], BF16) for _ in range(2)]
            rxg = [Res() for _ in range(2)]
            sl_ = [ar.alloc([128, 512], F32) for _ in range(2)]
            rsl = [Res() for _ in range(2)]
            hT = [ar.alloc([128, NH, 512], BF16) for _ in range(2)]
            rhT = [Res() for _ in range(2)]

            def load(g):
                k = g % 2
                P.dma("sp", xg[k], sc["xT2"][:, g * 512:(g + 1) * 512].rearrange("(k p) t -> p k t", p=128),
                      reads=[R["xT2"]], writes=[rxg[k]])

            load(0)
            it = 0
            for g in range(NG):
                if g + 1 < NG:
                    load(g + 1)
                k = g % 2
                for j in range(NH):
                    bG, bU = ((0, 1), (2, 3), (4, 5), (6, 7))[it % 4]
                    it += 1
                    for kc in range(8):
                        mm(pb[bG][:], wu[:, kc, j * 128:(j + 1) * 128], xg[k][:, kc, :], kc == 0, kc == 7,
                           [rw, rxg[k]], rb[bG], kc == 0)
                    for kc in range(8):
                        mm(pb[bU][:], wu[:, kc, FFN_H + j * 128:FFN_H + (j + 1) * 128], xg[k][:, kc, :], kc == 0,
                           kc == 7, [rw, rxg[k]], rb[bU], kc == 0)
                    s_, rs_ = sl_[j % 2], rsl[j % 2]
                    act(s_, pb[bG][:], AF.Silu, [rb[bG]], wr=[rs_])
                    tt("dve", hT[k][:, j, :], pb[bU][:], s_, ALU.mult, [rb[bU], rs_], wr=[rhT[k]] if j == 0 else (),
                       pw=() if j == 0 else [rhT[k]])
                P.dma("pool", sc["hT"][:, g * 512:(g + 1) * 512].rearrange("(k p) t -> p k t", p=128), hT[k],
                      reads=[rhT[k]], partial=[R["hT"]])
            P.barrier()

        x_src, r_xsrc = x_in, r_ext
        for l in range(L):
            sc = scr[l]
            R = sc["res"]
            if l == 0 and want("p0"):
                phase0(sc)
            if want("p1"):
                phase1(l, sc)
            if want("p2"):
                phase2(l, sc)
            if want("p3"):
                phase3(l, sc)
            if want("p4"):
                phase4(l, sc)
            if want("p5"):
                phase5a(l, sc)
                phase_out(l, sc, "mT", 8, w_out, 0, x_src, r_xsrc, sc["x1"], R["x1"], sc["xT1"], R["xT1"])
            if want("p6"):
                phase6(l, sc)
            if want("p7"):
                phase7a(l, sc)
                if l == L - 1:
                    phase_out(l, sc, "hT", FFN_H // 128, w_down, 2, sc["x2"], R["x2"], y_out, Res("y"), None, None)
                else:
                    nxt = scr[l + 1]
                    phase_out(l, sc, "hT", FFN_H // 128, w_down, 2, sc["x2"], R["x2"], sc["x3"], R["x3"],
                              nxt["xT0"], nxt["res"]["xT0"])
                    x_src, r_xsrc = sc["x3"], R["x3"]
        P.emit(es)
    return nc


def host_consts(S):
    half = 8
    inv_freq = np.power(np.float32(500000.0), -2.0 * np.arange(half, dtype=np.float32) / np.float32(16.0)).astype(np.float32)
    ang = np.arange(S, dtype=np.float32)[:, None] * inv_freq[None, :]
    rot = np.concatenate([np.cos(ang), np.sin(ang)], axis=1).astype(np.float32)
    k = np.arange(128)[:, None]
    q = np.arange(128)[None, :]
    maskd = (q >= k).astype(np.float32)
    prev = (k >= q).astype(np.float32)
    cur = (k <= q).astype(np.float32)
    maskc = np.concatenate([prev, cur, prev, cur], axis=1).astype(np.float32)
    return {"c_ident": np.eye(128, dtype=np.float32), "c_rot": rot, "c_maskd": maskd, "c_maskc": maskc}


def make_in_maps(inputs, S, L, n_cores):
    f = lambda a: np.ascontiguousarray(np.asarray(a, dtype=np.float32))
    shared = {
        "w_in": f(inputs["w_in"])[:L],
        "lam_qk": f(inputs["lam_qk"])[:L].reshape(L, 256),
        "diff_subln": f(inputs["diff_subln"])[:L],
        "rnn_par": np.ascontiguousarray(np.concatenate([
            f(inputs["conv_w"])[:L], f(inputs["conv_b"])[:L, None, :], f(inputs["gate_b"])[:L],
            f(inputs["lru_lambda"])[:L, None, :]], axis=1)),
        "gate_w": f(inputs["gate_w"])[:L],
        "w_br_a": f(inputs["w_br_a"])[:L], "w_br_b": f(inputs["w_br_b"])[:L], "w_br_c": f(inputs["w_br_c"])[:L],
        "w_out": f(inputs["w_out"])[:L],
        "lnp": np.ascontiguousarray(np.stack([f(inputs[k])[:L] for k in
                                              ("ln1_g", "ln1_b", "ln2_g", "ln2_b", "ln3_g", "ln3_b")], axis=1)),
        "xq": f(inputs["xq"])[:L], "xkv": f(inputs["xkv"])[:L], "xo": f(inputs["xo"])[:L],
        "w_up": f(inputs["w_up"])[:L], "w_down": f(inputs["w_down"])[:L],
    }
    shared.update(host_consts(S))
    x = f(inputs["x"])
    mem = f(inputs["mem"])
    maps = []
    for c in range(n_cores):
        m = dict(shared)
        m["x"] = np.ascontiguousarray(x[c, :S])
        m["mem"] = np.ascontiguousarray(mem[c])
        maps.append(m)
    return maps


_NC_CACHE = {}


def kernel(**inputs):
    key = (SEQ, DEPTH)
    if key not in _NC_CACHE:
        _NC_CACHE[key] = build(SEQ, DEPTH)
    nc = _NC_CACHE[key]
    maps = make_in_maps(inputs, SEQ, DEPTH, N_CORES)
    res = run_bass_kernel_spmd(nc, maps, core_ids=list(range(N_CORES)))
    return np.stack([np.asarray(r["y"], dtype=np.float32) for r in res.results], axis=0)
```
